# Optimizing a Trainium2 kernel written in Bass

```python
import math
import jax, jax.numpy as jnp
from jax import lax
import numpy as np

D_MODEL = 2048
BATCH = 4
SEQ = 8192
DEPTH = 4

CHUNK = 64
N_MIXERS = 2
Q_BLOCK = 128

MLA_HEADS = 16
QK_NOPE = 128
QK_ROPE = 64
V_HEAD = 128
Q_LORA = 512
KV_LORA = 512
ROPE_THETA = 10000.0
MLA_QK = QK_NOPE + QK_ROPE
MLA_WIDTH = MLA_HEADS * V_HEAD
MLA_IN = Q_LORA + KV_LORA + QK_ROPE + MLA_WIDTH

ML_HEADS = 8
ML_QK = 128
ML_V = 256
ML_QK_WIDTH = ML_HEADS * ML_QK
ML_WIDTH = ML_HEADS * ML_V
CONV_W = 4
ML_IN = 2 * ML_QK_WIDTH + 2 * ML_WIDTH + 2 * ML_HEADS + ML_WIDTH

ALPHA = (2 * DEPTH) ** 0.25
BETA = (8 * DEPTH) ** -0.25
EPS = 1e-6

kernel_name = 'hybrid_mla_mlstm_deepnorm'


def _rmsnorm(x, w):
    xf = x.astype(jnp.float32)
    y = xf * lax.rsqrt(jnp.mean(xf * xf, axis=-1, keepdims=True) + EPS)
    return y.astype(x.dtype) * w


def _layernorm(x, w, b):
    xf = x.astype(jnp.float32)
    mu = jnp.mean(xf, axis=-1, keepdims=True)
    var = jnp.mean(jnp.square(xf - mu), axis=-1, keepdims=True)
    return ((xf - mu) * lax.rsqrt(var + EPS)).astype(x.dtype) * w + b


def _rope_tables(positions, dtype):
    inv_freq = ROPE_THETA ** (-jnp.arange(0, QK_ROPE, 2, dtype=jnp.float32) / QK_ROPE)
    ang = positions.astype(jnp.float32)[..., None] * inv_freq
    return jnp.cos(ang).astype(dtype), jnp.sin(ang).astype(dtype)


def _rope(x, cos, sin):
    half = x.shape[-1] // 2
    x1, x2 = x[..., :half], x[..., half:]
    return jnp.concatenate([x1 * cos - x2 * sin, x1 * sin + x2 * cos], axis=-1)


def _chunk_causal_attention(q, k, v):
    b, s, h, dk = q.shape
    nb = s // Q_BLOCK
    scale = dk ** -0.5
    q_blocks = q.reshape(b, nb, Q_BLOCK, h, dk).transpose(1, 0, 2, 3, 4)
    key_chunk = jnp.arange(s) // CHUNK

    def one_block(args):
        qb, bi = args
        q_chunk = (bi * Q_BLOCK + jnp.arange(Q_BLOCK)) // CHUNK
        mask = key_chunk[None, :] <= q_chunk[:, None]
        sc = jnp.einsum('bqhd,bkhd->bhqk', qb, k).astype(jnp.float32) * scale
        sc = jnp.where(mask, sc, -jnp.inf)
        p = jax.nn.softmax(sc, axis=-1).astype(v.dtype)
        return jnp.einsum('bhqk,bkhd->bqhd', p, v)

    out = lax.map(one_block, (q_blocks, jnp.arange(nb)))
    return out.transpose(1, 0, 2, 3, 4).reshape(b, s, h, v.shape[-1])


def _mla_mixer(x, cos, sin, w_in, q_norm, w_qb, kv_norm, w_kvb, w_out):
    b, s, _ = x.shape
    proj = x @ w_in
    c_q, c_kv, k_rope, z = jnp.split(
        proj, [Q_LORA, Q_LORA + KV_LORA, Q_LORA + KV_LORA + QK_ROPE], axis=-1)
    q = (_rmsnorm(c_q, q_norm) @ w_qb).reshape(b, s, MLA_HEADS, MLA_QK)
    q_nope, q_rope = q[..., :QK_NOPE], q[..., QK_NOPE:]
    kv = (_rmsnorm(c_kv, kv_norm) @ w_kvb).reshape(b, s, MLA_HEADS, QK_NOPE + V_HEAD)
    k_nope, v = kv[..., :QK_NOPE], kv[..., QK_NOPE:]
    q_rope = _rope(q_rope, cos[:, :, None, :], sin[:, :, None, :])
    k_rope = _rope(k_rope, cos, sin)
    q = jnp.concatenate([q_nope, q_rope], axis=-1)
    k = jnp.concatenate(
        [k_nope, jnp.broadcast_to(k_rope[:, :, None, :], (b, s, MLA_HEADS, QK_ROPE))], axis=-1)
    attn = _chunk_causal_attention(q, k, v).reshape(b, s, MLA_WIDTH)
    return (attn * jax.nn.silu(z)) @ w_out


def _causal_depthwise_conv(x, w, bias):
    c = x.shape[-1]
    y = lax.conv_general_dilated(
        x, w[:, None, :], window_strides=(1,), padding=[(CONV_W - 1, 0)],
        dimension_numbers=('NWC', 'WIO', 'NWC'), feature_group_count=c)
    return y + bias


def _mlstm_chunkwise(q, k, v, i_pre, f_pre):
    b, s, h, dk = q.shape
    dv = v.shape[-1]
    nc = s // CHUNK
    f32 = jnp.float32

    def chunks(t):
        return t.astype(f32).reshape(b, nc, CHUNK, h, -1).transpose(1, 0, 3, 2, 4)

    def gate_chunks(t):
        return t.astype(f32).reshape(b, nc, CHUNK, h).transpose(1, 0, 3, 2)

    qc = chunks(q)
    kc = chunks(k) * (dk ** -0.5)
    vc = chunks(v)
    log_i = gate_chunks(i_pre)
    log_f = jax.nn.log_sigmoid(gate_chunks(f_pre))
    tril = jnp.tril(jnp.ones((CHUNK, CHUNK), dtype=bool))

    def step(carry, xs):
        c_st, n_st, m_st = carry
        qx, kx, vx, li, lf = xs
        bcum = jnp.cumsum(lf, axis=-1)
        dmat = bcum[..., :, None] - bcum[..., None, :] + li[..., None, :]
        dmat = jnp.where(tril, dmat, -jnp.inf)
        m_inter = bcum + m_st[..., None]
        m_t = jnp.maximum(m_inter, jnp.max(dmat, axis=-1))
        w_inter = jnp.exp(m_inter - m_t)
        s_qk = jnp.einsum('bhtd,bhsd->bhts', qx, kx) * jnp.exp(dmat - m_t[..., None])
        num = (w_inter[..., None] * jnp.einsum('bhtd,bhdv->bhtv', qx, c_st)
               + jnp.einsum('bhts,bhsv->bhtv', s_qk, vx))
        den = w_inter * jnp.einsum('bhtd,bhd->bht', qx, n_st) + jnp.sum(s_qk, axis=-1)
        h_out = num / jnp.maximum(jnp.abs(den), jnp.exp(-m_t))[..., None]
        b_last = bcum[..., -1]
        g = b_last[..., None] - bcum + li
        m_new = jnp.maximum(b_last + m_st, jnp.max(g, axis=-1))
        decay = jnp.exp(b_last + m_st - m_new)
        wk = jnp.exp(g - m_new[..., None])
        c_new = decay[..., None, None] * c_st + jnp.einsum('bhs,bhsd,bhsv->bhdv', wk, kx, vx)
        n_new = decay[..., None] * n_st + jnp.einsum('bhs,bhsd->bhd', wk, kx)
        return (c_new, n_new, m_new), h_out

    init = (jnp.zeros((b, h, dk, dv), f32), jnp.zeros((b, h, dk), f32), jnp.zeros((b, h), f32))
    _, hs = lax.scan(step, init, (qc, kc, vc, log_i, log_f))
    return hs.transpose(1, 0, 3, 2, 4).reshape(b, s, h, dv)


def _mlstm_mixer(x, w_in, conv_w, conv_b, gate_b, head_norm, w_out):
    b, s, _ = x.shape
    proj = x @ w_in
    o1 = 2 * ML_QK_WIDTH
    o2 = o1 + ML_WIDTH
    o3 = o2 + ML_WIDTH
    o4 = o3 + 2 * ML_HEADS
    qk, v, o, gates, z = jnp.split(proj, [o1, o2, o3, o4], axis=-1)
    qk = jax.nn.silu(_causal_depthwise_conv(qk, conv_w, conv_b))
    q, k = jnp.split(qk, 2, axis=-1)
    gates = gates + gate_b
    i_pre, f_pre = gates[..., :ML_HEADS], gates[..., ML_HEADS:]
    cell = _mlstm_chunkwise(q.reshape(b, s, ML_HEADS, ML_QK), k.reshape(b, s, ML_HEADS, ML_QK),
                            v.reshape(b, s, ML_HEADS, ML_V), i_pre, f_pre)
    cell = _rmsnorm(cell.astype(x.dtype), head_norm.reshape(ML_HEADS, ML_V)).reshape(b, s, ML_WIDTH)
    hcell = jax.nn.sigmoid(o) * cell
    return (hcell * jax.nn.silu(z)) @ w_out


def setup_inputs(seed: int = 0) -> dict:
    key = jax.random.key(seed)
    ks = jax.random.split(key, 20)
    n_a = (DEPTH + N_MIXERS - 1) // N_MIXERS
    n_b = DEPTH // N_MIXERS

    def nrm(k, shape, scale):
        return jax.random.normal(k, shape, jnp.float32) * scale

    x = nrm(ks[0], (BATCH, SEQ, D_MODEL), 1.0)
    offsets = jax.random.randint(ks[1], (BATCH, 1), 0, 4096, dtype=jnp.int32)
    positions = offsets + jnp.arange(SEQ, dtype=jnp.int32)[None, :]

    mla_w_in = nrm(ks[2], (n_a, D_MODEL, MLA_IN), D_MODEL ** -0.5)
    mla_q_norm = 1.0 + nrm(ks[3], (n_a, Q_LORA), 0.02)
    mla_w_qb = nrm(ks[4], (n_a, Q_LORA, MLA_HEADS * MLA_QK), Q_LORA ** -0.5)
    mla_kv_norm = 1.0 + nrm(ks[5], (n_a, KV_LORA), 0.02)
    mla_w_kvb = nrm(ks[6], (n_a, KV_LORA, MLA_HEADS * (QK_NOPE + V_HEAD)), KV_LORA ** -0.5)
    mla_w_out = nrm(ks[7], (n_a, MLA_WIDTH, D_MODEL), BETA * MLA_WIDTH ** -0.5)

    ml_w_in = nrm(ks[8], (n_b, D_MODEL, ML_IN), D_MODEL ** -0.5)
    ml_conv_w = nrm(ks[9], (n_b, CONV_W, 2 * ML_QK_WIDTH), CONV_W ** -0.5)
    ml_conv_b = nrm(ks[10], (n_b, 2 * ML_QK_WIDTH), 0.02)
    i_bias = nrm(ks[11], (n_b, ML_HEADS), 0.1)
    f_bias = jnp.linspace(3.0, 6.0, ML_HEADS, dtype=jnp.float32)[None, :] + nrm(ks[12], (n_b, ML_HEADS), 0.1)
    ml_gate_b = jnp.concatenate([i_bias, f_bias], axis=-1)
    ml_head_norm = 1.0 + nrm(ks[13], (n_b, ML_WIDTH), 0.02)
    ml_w_out = nrm(ks[14], (n_b, ML_WIDTH, D_MODEL), BETA * ML_WIDTH ** -0.5)

    ln_w = 1.0 + nrm(ks[15], (DEPTH, D_MODEL), 0.02)
    ln_b = nrm(ks[16], (DEPTH, D_MODEL), 0.02)
    return {'x': x, 'positions': positions,
            'mla_w_in': mla_w_in, 'mla_q_norm': mla_q_norm, 'mla_w_qb': mla_w_qb,
            'mla_kv_norm': mla_kv_norm, 'mla_w_kvb': mla_w_kvb, 'mla_w_out': mla_w_out,
            'ml_w_in': ml_w_in, 'ml_conv_w': ml_conv_w, 'ml_conv_b': ml_conv_b,
            'ml_gate_b': ml_gate_b, 'ml_head_norm': ml_head_norm, 'ml_w_out': ml_w_out,
            'ln_w': ln_w, 'ln_b': ln_b}


def reference(x, positions, mla_w_in, mla_q_norm, mla_w_qb, mla_kv_norm, mla_w_kvb, mla_w_out,
              ml_w_in, ml_conv_w, ml_conv_b, ml_gate_b, ml_head_norm, ml_w_out, ln_w, ln_b):
    cos, sin = _rope_tables(positions, x.dtype)
    for layer in range(DEPTH):
        j = layer // N_MIXERS
        if layer % N_MIXERS == 0:
            y = _mla_mixer(x, cos, sin, mla_w_in[j], mla_q_norm[j], mla_w_qb[j],
                           mla_kv_norm[j], mla_w_kvb[j], mla_w_out[j])
        else:
            y = _mlstm_mixer(x, ml_w_in[j], ml_conv_w[j], ml_conv_b[j], ml_gate_b[j],
                             ml_head_norm[j], ml_w_out[j])
        x = _layernorm(ALPHA * x + y, ln_w[layer], ln_b[layer])
    return x
```

```python
import numpy as np
import concourse.bass as bass
import concourse.mybir as mybir
from contextlib import ExitStack

F32 = mybir.dt.float32
BF16 = mybir.dt.bfloat16
I32 = mybir.dt.int32
ALU = mybir.AluOpType
AF = mybir.ActivationFunctionType
AX = mybir.AxisListType

COMPUTE = ('pe', 'act', 'dve', 'pool')
ENGS = ('pe', 'act', 'dve', 'pool', 'sp')


class Res:
    __slots__ = ('name', 'w', 'rs', 'excl')

    def __init__(self, name, excl=False):
        self.name = name
        self.excl = excl
        self.w = None
        self.rs = []


class Prog:
    def __init__(self, nc, tag):
        self.nc = nc
        self.tag = tag
        self.ops = {e: [] for e in ENGS}
        self.known = {e: {} for e in ENGS}
        self.dma_val = {}
        self.dma_last = {}
        self.marked = {e: set() for e in COMPUTE}
        self.enabled = True

    def _need(self, eng, ev, waits):
        if ev is None:
            return
        if ev[0] == 'c':
            if ev[1] == eng and eng == 'pe':
                return
            if ev[1] == eng and eng == 'sp':
                return
            k = ('c', ev[1])
        else:
            k = ('d', ev[1])
        if self.known[eng].get(k, -1) >= ev[2]:
            return
        self.known[eng][k] = ev[2]
        waits.append(ev)
        if ev[0] == 'c':
            self.marked[ev[1]].add(ev[2])

    def _deps(self, eng, reads, writes):
        waits = []
        for r in reads:
            self._need(eng, r.w, waits)
            if r.excl:
                for ev in r.rs:
                    if not (ev[0] == 'c' and ev[1] == eng):
                        self._need(eng, ev, waits)
        for w in writes:
            self._need(eng, w.w, waits)
            for ev in w.rs:
                self._need(eng, ev, waits)
        return waits

    def op(self, eng, fn, reads=(), writes=()):
        if not self.enabled:
            return None
        waits = self._deps(eng, reads, writes)
        pos = len(self.ops[eng])
        ev = ('c', eng, pos)
        self.ops[eng].append(dict(waits=waits, fn=fn, ev=ev, dma=None))
        for r in reads:
            r.rs.append(ev)
        for w in writes:
            w.w = ev
            w.rs = []
        return ev

    def dma(self, q, key, out, in_, reads=(), writes=()):
        if not self.enabled:
            return None
        waits = self._deps(q, reads, writes)
        self._need(q, self.dma_last.get(key), waits)
        v = self.dma_val.get(key, 0) + 16
        self.dma_val[key] = v
        ev = ('d', key, v)
        self.dma_last[key] = ev
        self.ops[q].append(dict(waits=waits, fn=lambda e: e.dma_start(out=out, in_=in_), ev=None, dma=(key, v)))
        for r in reads:
            r.rs.append(ev)
        for w in writes:
            w.w = ev
            w.rs = []
        return ev

    def barrier(self):
        last = {}
        for e in COMPUTE:
            for pos in range(len(self.ops[e]) - 1, -1, -1):
                if self.ops[e][pos]['ev'] is not None:
                    last[e] = self.ops[e][pos]['ev']
                    break
        dm = list(self.dma_last.values())
        for e in ENGS:
            w = []
            for e2, ev in last.items():
                if e2 != e:
                    self._need(e, ev, w)
            for ev in dm:
                self._need(e, ev, w)
            if w:
                self.ops[e].append(dict(waits=w, fn=None, ev=None, dma=None))

    def emit(self):
        nc = self.nc
        engobj = {'pe': 'tensor', 'act': 'scalar', 'dve': 'vector', 'pool': 'gpsimd', 'sp': 'sync'}
        fin = []
        for key, ev in self.dma_last.items():
            self._need('sp', ev, fin)
        self.ops['sp'].append(dict(waits=fin, fn=None, ev=None, dma=None))
        val = {}
        for e in COMPUTE:
            c = 0
            for pos in range(len(self.ops[e])):
                if pos in self.marked[e]:
                    c += 1
                    val[(e, pos)] = c
        with ExitStack() as st:
            csem = {e: st.enter_context(nc.semaphore(f"{self.tag}_c_{e}")) for e in COMPUTE}
            dsem = {k: st.enter_context(nc.semaphore(f"{self.tag}_d_{k}")) for k in self.dma_val}
            block = st.enter_context(nc.Block())

            def run(e, eng):
                for o in self.ops[e]:
                    for ev in o['waits']:
                        if ev[0] == 'c':
                            eng.wait_ge(csem[ev[1]], val[(ev[1], ev[2])])
                        else:
                            eng.wait_ge(dsem[ev[1]], ev[2])
                    if o['fn'] is None:
                        continue
                    ins = o['fn'](eng)
                    if o['dma'] is not None:
                        ins.then_inc(dsem[o['dma'][0]], 16)
                    elif o['ev'] is not None and (e, o['ev'][2]) in val:
                        ins.then_inc(csem[e], 1)

            for e in ENGS:
                if not self.ops[e]:
                    continue
                getattr(block, engobj[e])(lambda eng, e=e: run(e, eng))
        n = {e: len(self.ops[e]) for e in ENGS}
        return n


ALPHA = 8 ** 0.25
EPS = 1e-6
D = 2048
NCH = 16


def phase_B(nc, tag, S, gT, xT, w_out, lnw, lnb, ones_bf, outT, TN=256):
    pg = Prog(nc, tag)
    nt = S // TN
    with ExitStack() as st:
        def sb(name, shape, dt):
            return st.enter_context(nc.sbuf_tensor(f"{tag}_{name}", shape, dt))

        def ps(name, shape, dt=F32):
            return st.enter_context(nc.psum_tensor(f"{tag}_{name}", shape, dt))

        W = sb("W", [128, NCH, D], BF16)
        G = [sb(f"G{i}", [128, NCH, TN], BF16) for i in range(2)]
        XV = [sb(f"XV{i}", [128, NCH, TN], F32) for i in range(2)]
        Vb = sb("Vb", [128, NCH, TN], BF16)
        Sq = sb("Sq", [128, NCH, TN], BF16)
        ones = sb("ones", [128, 128], BF16)
        lw = sb("lw", [128, NCH], F32)
        lb = sb("lb", [128, NCH], F32)
        mu = sb("mu", [128, TN], F32)
        musq = sb("musq", [128, TN], F32)
        rstd = sb("rstd", [128, TN], F32)
        tmp = [sb(f"tmp{i}", [128, TN], F32) for i in range(2)]
        PY = [ps(f"py{i}", [128, 512]) for i in range(3)]
        PMU = ps("pmu", [128, 512])
        PE2 = ps("pe2", [128, 512])

        rW = [Res(f"W{k}") for k in range(NCH)]
        rG = [Res("G0"), Res("G1")]
        rXV = [[Res(f"XV{i}_{c}") for c in range(NCH)] for i in range(2)]
        rVb = [Res(f"Vb{c}") for c in range(NCH)]
        rSq = [Res(f"Sq{c}") for c in range(NCH)]
        rones, rlw, rlb = Res("ones"), Res("lw"), Res("lb")
        rmu, rmusq, rrstd = Res("mu"), Res("musq"), Res("rstd")
        rtmp = [Res("tmp0"), Res("tmp1")]
        rPY = [Res(f"py{i}", excl=True) for i in range(3)]
        rPMU, rPE2 = Res("pmu", excl=True), Res("pe2", excl=True)
        rout = Res("outT")

        pg.dma('sp', 'ones', ones[:, :], ones_bf, writes=[rones])
        pg.dma('sp', 'lw', lw[:, :], lnw, writes=[rlw])
        pg.dma('sp', 'lb', lb[:, :], lnb, writes=[rlb])
        w_v = w_out.rearrange("(c p) n -> p c n", p=128)
        for k in range(NCH):
            pg.dma('pool', f'W{k % 4}', W[:, k, :], w_v[:, k, :], writes=[rW[k]])

        gv = gT.rearrange("(c p) t -> p c t", p=128)
        xv = xT.rearrange("(c p) t -> p c t", p=128)
        ov = outT.rearrange("(c p) t -> p c t", p=128)

        def load(t):
            i = t % 2
            sl = slice(t * TN, (t + 1) * TN)
            pg.dma('sp', f'G{i}', G[i][:, :, :], gv[:, :, sl], writes=[rG[i]])
            for h in range(2):
                cs = slice(h * 8, (h + 1) * 8)
                pg.dma('sp', f'X{i}{h}', XV[i][:, cs, :], xv[:, cs, sl], writes=rXV[i][h * 8:(h + 1) * 8])

        load(0)
        pyc = 0
        for t in range(nt):
            i = t % 2
            if t + 1 < nt:
                load(t + 1)
            for oc in range(NCH):
                b = pyc % 3
                pyc += 1
                for k in range(NCH):
                    pg.op('pe', lambda e, b=b, k=k, oc=oc, i=i: e.matmul(
                        PY[b][:, :TN], W[:, k, oc * 128:(oc + 1) * 128], G[i][:, k, :],
                        start=(k == 0), stop=(k == NCH - 1)),
                        reads=[rW[k], rG[i]], writes=[rPY[b]])
                pg.op('dve', lambda e, b=b, oc=oc, i=i: e.scalar_tensor_tensor(
                    out=XV[i][:, oc, :], in0=XV[i][:, oc, :], scalar=float(ALPHA), in1=PY[b][:, :TN],
                    op0=ALU.mult, op1=ALU.add),
                    reads=[rPY[b]], writes=[rXV[i][oc]])
                pg.op('pool', lambda e, oc=oc, i=i: e.tensor_copy(out=Vb[:, oc, :], in_=XV[i][:, oc, :]),
                      reads=[rXV[i][oc]], writes=[rVb[oc]])
                pg.op('act', lambda e, oc=oc, i=i: e.activation(out=Sq[:, oc, :], in_=XV[i][:, oc, :], func=AF.Square),
                      reads=[rXV[i][oc]], writes=[rSq[oc]])
            for oc in range(NCH):
                pg.op('pe', lambda e, oc=oc: e.matmul(PMU[:, :TN], ones[:, :], Vb[:, oc, :],
                                                      start=(oc == 0), stop=(oc == NCH - 1)),
                      reads=[rones, rVb[oc]], writes=[rPMU])
            for oc in range(NCH):
                pg.op('pe', lambda e, oc=oc: e.matmul(PE2[:, :TN], ones[:, :], Sq[:, oc, :],
                                                      start=(oc == 0), stop=(oc == NCH - 1)),
                      reads=[rones, rSq[oc]], writes=[rPE2])
            pg.op('act', lambda e: e.activation(out=mu[:, :], in_=PMU[:, :TN], func=AF.Copy, scale=1.0 / D),
                  reads=[rPMU], writes=[rmu])
            pg.op('pool', lambda e: e.tensor_tensor(out=musq[:, :], in0=mu[:, :], in1=mu[:, :], op=ALU.mult),
                  reads=[rmu], writes=[rmusq])
            pg.op('dve', lambda e: e.scalar_tensor_tensor(out=rstd[:, :], in0=PE2[:, :TN], scalar=1.0 / D, in1=musq[:, :],
                                                          op0=ALU.mult, op1=ALU.subtract),
                  reads=[rPE2, rmusq], writes=[rrstd])
            pg.op('act', lambda e: e.activation(out=rstd[:, :], in_=rstd[:, :], func=AF.Sqrt, bias=float(EPS)),
                  reads=[rrstd], writes=[rrstd])
            pg.op('dve', lambda e: e.reciprocal(out=rstd[:, :], in_=rstd[:, :]),
                  reads=[rrstd], writes=[rrstd])
            for oc in range(NCH):
                tb = oc % 2
                pg.op('pool', lambda e, oc=oc, i=i, tb=tb: e.tensor_tensor(out=tmp[tb][:, :], in0=XV[i][:, oc, :], in1=mu[:, :],
                                                                             op=ALU.subtract),
                      reads=[rXV[i][oc], rmu], writes=[rtmp[tb]])
                pg.op('dve', lambda e, oc=oc, tb=tb: e.scalar_tensor_tensor(out=tmp[tb][:, :], in0=tmp[tb][:, :],
                                                                          scalar=lw[:, oc:oc + 1], in1=rstd[:, :],
                                                                          op0=ALU.mult, op1=ALU.mult),
                      reads=[rtmp[tb], rlw, rrstd], writes=[rtmp[tb]])
                pg.op('act', lambda e, oc=oc, i=i, tb=tb: e.activation(out=XV[i][:, oc, :], in_=tmp[tb][:, :], func=AF.Identity,
                                                                       bias=lb[:, oc:oc + 1]),
                      reads=[rtmp[tb], rlb], writes=[rXV[i][oc]])
            sl = slice(t * TN, (t + 1) * TN)
            for h in range(2):
                cs = slice(h * 8, (h + 1) * 8)
                pg.dma('sp', f'O{i}{h}', ov[:, cs, sl], XV[i][:, cs, :], reads=rXV[i][h * 8:(h + 1) * 8])
        n = pg.emit()
    return n

import math

EPS = 1e-6
NCH = 16
TWO_PI = 2.0 * math.pi
C1 = 6.28125
C2 = TWO_PI - C1


def phase_MLA(nc, tag, S, NH, xT, pos, W1d, Wqd, Wkvd, qnd, kvnd, cst, ones_bf, scr, gT_out, do_a2=True, a1_parts=None):
    pg = Prog(nc, tag)
    P = (lambda x: True) if a1_parts is None else (lambda x: x in a1_parts)
    TA, TQ = 256, 512
    NS = TA // 128
    nta = S // TA
    nt = S // TQ
    NB = S // 128
    ZC = NH
    W1C = 1152 + NH * 128
    scale = 192.0 ** -0.5
    with ExitStack() as st:
        def sb(name, shape, dt):
            return st.enter_context(nc.sbuf_tensor(f"{tag}_{name}", shape, dt))

        PS = [st.enter_context(nc.psum_tensor(f"{tag}_ps{i}", [128, 512], F32)) for i in range(8)]
        rPS = [Res(f"ps{i}", excl=True) for i in range(8)]
        psc = [0]

        def nextps():
            b = psc[0] % 8
            psc[0] += 1
            return b

        ones = sb("ones", [128, 128], BF16)
        C = sb("cst", [128, 4], F32)
        qn = sb("qn", [128, 4], F32)
        kvn = sb("kvn", [128, 4], F32)
        rones, rC, rqn, rkvn = Res("ones"), Res("C"), Res("qn"), Res("kvn")
        pg.dma('sp', 'ones', ones[:, :], ones_bf, writes=[rones])
        pg.dma('sp', 'cst', C[:, :], cst, writes=[rC])
        pg.dma('sp', 'qn', qn[:, :], qnd, writes=[rqn])
        pg.dma('sp', 'kvn', kvn[:, :], kvnd, writes=[rkvn])

        with ExitStack() as st1:
            def sb1(name, shape, dt):
                return st1.enter_context(nc.sbuf_tensor(f"{tag}_{name}", shape, dt))
            W1 = sb1("W1", [128, NCH, W1C], BF16)
            Wq = sb1("Wq", [128, 4, NH * 256], BF16)
            Wkv = sb1("Wkv", [128, 4, NH * 256], BF16)
            X = [sb1(f"X{i}", [128, NCH, TA], BF16) for i in range(2)]
            posi = sb1("posi", [64, TA], I32)
            ang = sb1("ang", [64, TA], F32)
            kf = sb1("kf", [64, TA], F32)
            ki = sb1("ki", [64, TA], I32)
            rr = sb1("rr", [64, TA], F32)
            CC = sb1("CC", [64, TA], F32)
            SS = sb1("SS", [64, TA], F32)
            sq = sb1("sq", [128, 4, TA], BF16)
            cs = sb1("cs", [128, 4, TA], F32)
            rn = sb1("rn", [128, TA], F32)
            cqn = sb1("cqn", [128, 4, TA], BF16)
            ckvn = sb1("ckvn", [128, 4, TA], BF16)
            t1 = [sb1(f"t1_{i}", [64, TA], F32) for i in range(2)]
            t2 = [sb1(f"t2_{i}", [64, TA], F32) for i in range(2)]
            krs = [sb1(f"krs{i}", [64, TA], BF16) for i in range(2)]
            szs = [sb1(f"szs{i}", [128, ZC, TA], BF16) for i in range(2)]
            qns = [sb1(f"qns{i}", [128, NH, TA], BF16) for i in range(2)]
            qrs = [sb1(f"qrs{i}", [64, NH, TA], BF16) for i in range(2)]
            kns = [sb1(f"kns{i}", [128, NH, TA], BF16) for i in range(2)]
            vs = [sb1(f"vs{i}", [128, NS, NH * 128], BF16) for i in range(2)]

            rW1 = [Res(f"W1_{k}") for k in range(NCH)]
            rWq, rWkv = Res("Wq"), Res("Wkv")
            rX = [Res("X0"), Res("X1")]
            rtab = Res("tabtmp")
            rCC, rSS = Res("CC"), Res("SS")
            rsq = [Res(f"sq{c}") for c in range(4)]
            rcs = [Res(f"cs{c}") for c in range(4)]
            rrn = Res("rn")
            rcqn = [Res(f"cqn{c}") for c in range(4)]
            rckvn = [Res(f"ckvn{c}") for c in range(4)]
            rt1 = [Res("t1_0"), Res("t1_1")]
            rt2 = [Res("t2_0"), Res("t2_1")]
            rkrs = [Res("krs0"), Res("krs1")]
            rszs = [[Res(f"szs{i}_{c}") for c in range(ZC)] for i in range(2)]
            rqns = [[Res(f"qns{i}_{h}") for h in range(NH)] for i in range(2)]
            rqrs = [[Res(f"qrs{i}_{h}") for h in range(NH)] for i in range(2)]
            rkns = [[Res(f"kns{i}_{h}") for h in range(NH)] for i in range(2)]
            rvs = [[Res(f"vs{i}_{s}") for s in range(NS)] for i in range(2)]

            w1v = W1d.rearrange("(c p) n -> p c n", p=128)
            for k in range(NCH):
                pg.dma('pool', f'W1_{k % 4}', W1[:, k, :], w1v[:, k, :], writes=[rW1[k]])
            pg.dma('pool', 'Wq', Wq[:, :, :], Wqd.rearrange("(c p) n -> p c n", p=128), writes=[rWq])
            pg.dma('pool', 'Wkv', Wkv[:, :, :], Wkvd.rearrange("(c p) n -> p c n", p=128), writes=[rWkv])
            xv = xT.rearrange("(c p) t -> p c t", p=128)

            def loadx(t):
                i = t % 2
                sl = slice(t * TA, (t + 1) * TA)
                for q4 in range(4):
                    cs_ = slice(q4 * 4, (q4 + 1) * 4)
                    pg.dma('pool', f'X{i}', X[i][:, cs_, :], xv[:, cs_, sl], writes=[rX[i]])

            loadx(0)
            for t in range(nta):
                i = t % 2
                sl = slice(t * TA, (t + 1) * TA)
                if t + 1 < nta:
                    loadx(t + 1)
                pg.enabled = P('tab')
                pg.dma('sp', 'posi', posi[:, :], pos[sl].partition_broadcast(64), writes=[rtab])
                pg.op('dve', lambda e: e.tensor_copy(out=ang[:, :], in_=posi[:, :]), reads=[rtab], writes=[rtab])
                pg.op('dve', lambda e: e.tensor_scalar(out=ang[:, :], in0=ang[:, :], scalar1=C[0:64, 0:1], scalar2=None, op0=ALU.mult),
                      reads=[rtab, rC], writes=[rtab])
                pg.op('dve', lambda e: e.tensor_scalar(out=kf[:, :], in0=ang[:, :], scalar1=float(1.0 / TWO_PI), scalar2=None, op0=ALU.mult),
                      reads=[rtab], writes=[rtab])
                pg.op('dve', lambda e: e.tensor_copy(out=ki[:, :], in_=kf[:, :]), reads=[rtab], writes=[rtab])
                pg.op('dve', lambda e: e.tensor_copy(out=kf[:, :], in_=ki[:, :]), reads=[rtab], writes=[rtab])
                pg.op('dve', lambda e: e.scalar_tensor_tensor(out=rr[:, :], in0=kf[:, :], scalar=float(-C1), in1=ang[:, :],
                                                              op0=ALU.mult, op1=ALU.add), reads=[rtab], writes=[rtab])
                pg.op('dve', lambda e: e.scalar_tensor_tensor(out=rr[:, :], in0=kf[:, :], scalar=float(-C2), in1=rr[:, :],
                                                              op0=ALU.mult, op1=ALU.add), reads=[rtab], writes=[rtab])
                pg.op('dve', lambda e: e.tensor_scalar(out=kf[:, :], in0=rr[:, :], scalar1=float(math.pi), scalar2=float(-TWO_PI),
                                                       op0=ALU.is_gt, op1=ALU.mult), reads=[rtab], writes=[rtab])
                pg.op('dve', lambda e: e.tensor_tensor(out=rr[:, :], in0=rr[:, :], in1=kf[:, :], op=ALU.add), reads=[rtab], writes=[rtab])
                pg.op('dve', lambda e: e.tensor_scalar(out=kf[:, :], in0=rr[:, :], scalar1=float(-math.pi), scalar2=float(TWO_PI),
                                                       op0=ALU.is_lt, op1=ALU.mult), reads=[rtab], writes=[rtab])
                pg.op('dve', lambda e: e.tensor_tensor(out=rr[:, :], in0=rr[:, :], in1=kf[:, :], op=ALU.add), reads=[rtab], writes=[rtab])
                pg.op('act', lambda e: e.activation(out=SS[:, :], in_=rr[:, :], func=AF.Sin, scale=C[0:64, 1:2]),
                      reads=[rtab, rC], writes=[rSS])
                pg.op('dve', lambda e: e.scalar_tensor_tensor(out=kf[:, :], in0=rr[:, :], scalar=-1.0, in1=rr[:, :],
                                                              op0=ALU.mult, op1=ALU.max), reads=[rtab], writes=[rtab])
                pg.op('act', lambda e: e.activation(out=CC[:, :], in_=kf[:, :], func=AF.Sin, scale=-1.0, bias=C[0:64, 2:3]),
                      reads=[rtab, rC], writes=[rCC])

                pg.enabled = P('c')
                for which in range(2):
                    nrm = qn if which == 0 else kvn
                    rnrm = rqn if which == 0 else rkvn
                    dst = cqn if which == 0 else ckvn
                    rdst = rcqn if which == 0 else rckvn
                    for c in range(4):
                        b = nextps()
                        col = which * 512 + c * 128
                        for k in range(NCH):
                            pg.op('pe', lambda e, b=b, k=k, col=col, i=i: e.matmul(
                                PS[b][:, 0:TA], W1[:, k, col:col + 128], X[i][:, k, :], start=(k == 0), stop=(k == NCH - 1)),
                                reads=[rW1[k], rX[i]], writes=[rPS[b]])
                        pg.op('act', lambda e, b=b, c=c: e.activation(out=cs[:, c, :], in_=PS[b][:, 0:TA], func=AF.Copy),
                              reads=[rPS[b]], writes=[rcs[c]])
                        pg.op('act', lambda e, c=c: e.activation(out=sq[:, c, :], in_=cs[:, c, :], func=AF.Square),
                              reads=[rcs[c]], writes=[rsq[c]])
                    b = nextps()
                    for c in range(4):
                        pg.op('pe', lambda e, b=b, c=c: e.matmul(PS[b][:, 0:TA], ones[:, :], sq[:, c, :], start=(c == 0), stop=(c == 3)),
                              reads=[rones, rsq[c]], writes=[rPS[b]])
                    pg.op('act', lambda e, b=b: e.activation(out=rn[:, :], in_=PS[b][:, 0:TA], func=AF.Sqrt, scale=1.0 / 512, bias=C[:, 3:4]),
                          reads=[rPS[b]], writes=[rrn])
                    pg.op('dve', lambda e: e.reciprocal(out=rn[:, :], in_=rn[:, :]), reads=[rrn], writes=[rrn])
                    for c in range(4):
                        pg.op('dve', lambda e, c=c, dst=dst, nrm=nrm: e.scalar_tensor_tensor(out=dst[:, c, :], in0=cs[:, c, :], scalar=nrm[:, c:c + 1],
                                                                                            in1=rn[:, :], op0=ALU.mult, op1=ALU.mult),
                              reads=[rcs[c], rrn, rnrm], writes=[rdst[c]])

                pg.enabled = P('kr')
                b1 = nextps()
                b2 = nextps()
                for (b, col) in ((b1, 1024), (b2, 1088)):
                    for k in range(NCH):
                        pg.op('pe', lambda e, b=b, k=k, col=col, i=i: e.matmul(
                            PS[b][0:64, 0:TA], W1[:, k, col:col + 64], X[i][:, k, :], start=(k == 0), stop=(k == NCH - 1)),
                            reads=[rW1[k], rX[i]], writes=[rPS[b]])
                pg.op('dve', lambda e, b1=b1: e.tensor_tensor(out=t1[0][:, :], in0=PS[b1][0:64, 0:TA], in1=CC[:, :], op=ALU.mult),
                      reads=[rPS[b1], rCC], writes=[rt1[0]])
                pg.op('dve', lambda e, b2=b2: e.tensor_tensor(out=t2[0][:, :], in0=PS[b2][0:64, 0:TA], in1=SS[:, :], op=ALU.mult),
                      reads=[rPS[b2], rSS], writes=[rt2[0]])
                pg.op('pool', lambda e, i=i: e.tensor_tensor(out=krs[i][:, :], in0=t1[0][:, :], in1=t2[0][:, :], op=ALU.add),
                      reads=[rt1[0], rt2[0]], writes=[rkrs[i]])
                pg.dma('sp', f'krs{i}', scr['kr'][:, sl], krs[i][:, :], reads=[rkrs[i]])

                pg.enabled = P('z')
                for c in range(ZC):
                    b = nextps()
                    col = 1152 + c * 128
                    for k in range(NCH):
                        pg.op('pe', lambda e, b=b, k=k, col=col, i=i: e.matmul(
                            PS[b][:, 0:TA], W1[:, k, col:col + 128], X[i][:, k, :], start=(k == 0), stop=(k == NCH - 1)),
                            reads=[rW1[k], rX[i]], writes=[rPS[b]])
                    pg.op('act', lambda e, b=b, c=c, i=i: e.activation(out=szs[i][:, c, :], in_=PS[b][:, 0:TA], func=AF.Silu),
                          reads=[rPS[b]], writes=[rszs[i][c]])
                pg.dma('sp', f'szs{i}', scr['sz'].rearrange("(c p) t -> p c t", p=128)[:, :, sl], szs[i][:, :, :], reads=rszs[i])

                pg.enabled = P('q')
                for h in range(NH):
                    bn, br, bs = nextps(), nextps(), nextps()
                    for (b, col, m) in ((bn, h * 256, 128), (br, h * 256 + 128, 64), (bs, h * 256 + 192, 64)):
                        for k in range(4):
                            pg.op('pe', lambda e, b=b, k=k, col=col, m=m: e.matmul(
                                PS[b][0:m, 0:TA], Wq[:, k, col:col + m], cqn[:, k, :], start=(k == 0), stop=(k == 3)),
                                reads=[rWq, rcqn[k]], writes=[rPS[b]])
                    pg.op('act', lambda e, bn=bn, h=h, i=i: e.activation(out=qns[i][:, h, :], in_=PS[bn][:, 0:TA], func=AF.Copy),
                          reads=[rPS[bn]], writes=[rqns[i][h]])
                    tb = h % 2
                    pg.op('dve', lambda e, br=br, tb=tb: e.tensor_tensor(out=t1[tb][:, :], in0=PS[br][0:64, 0:TA], in1=CC[:, :], op=ALU.mult),
                          reads=[rPS[br], rCC], writes=[rt1[tb]])
                    pg.op('dve', lambda e, bs=bs, tb=tb: e.tensor_tensor(out=t2[tb][:, :], in0=PS[bs][0:64, 0:TA], in1=SS[:, :], op=ALU.mult),
                          reads=[rPS[bs], rSS], writes=[rt2[tb]])
                    pg.op('pool', lambda e, h=h, i=i, tb=tb: e.tensor_tensor(out=qrs[i][:, h, :], in0=t1[tb][:, :], in1=t2[tb][:, :], op=ALU.add),
                          reads=[rt1[tb], rt2[tb]], writes=[rqrs[i][h]])
                pg.dma('sp', f'qns{i}', scr['qn'].rearrange("h p t -> p h t")[:, :, sl], qns[i][:, :, :], reads=rqns[i])
                pg.dma('sp', f'qrs{i}', scr['qr'].rearrange("h p t -> p h t")[:, :, sl], qrs[i][:, :, :], reads=rqrs[i])

                pg.enabled = P('kn')
                for h in range(NH):
                    b = nextps()
                    for k in range(4):
                        pg.op('pe', lambda e, b=b, k=k, h=h: e.matmul(
                            PS[b][:, 0:TA], Wkv[:, k, h * 128:(h + 1) * 128], ckvn[:, k, :], start=(k == 0), stop=(k == 3)),
                            reads=[rWkv, rckvn[k]], writes=[rPS[b]])
                    pg.op('dve', lambda e, b=b, h=h, i=i: e.tensor_copy(out=kns[i][:, h, :], in_=PS[b][:, 0:TA]),
                          reads=[rPS[b]], writes=[rkns[i][h]])
                pg.dma('sp', f'kns{i}', scr['kn'].rearrange("h p t -> p h t")[:, :, sl], kns[i][:, :, :], reads=rkns[i])

                pg.enabled = P('v')
                for s in range(NS):
                    for n in range(NH * 128 // 512):
                        b = nextps()
                        for k in range(4):
                            pg.op('pe', lambda e, b=b, k=k, s=s, n=n: e.matmul(
                                PS[b][:, :], ckvn[:, k, s * 128:(s + 1) * 128], Wkv[:, k, NH * 128 + n * 512: NH * 128 + (n + 1) * 512],
                                start=(k == 0), stop=(k == 3)),
                                reads=[rWkv, rckvn[k]], writes=[rPS[b]])
                        pg.op('act', lambda e, b=b, s=s, n=n, i=i: e.activation(out=vs[i][:, s, n * 512:(n + 1) * 512], in_=PS[b][:, :], func=AF.Copy),
                              reads=[rPS[b]], writes=[rvs[i][s]])
                pg.dma('sp', f'vs{i}', scr['v'].rearrange("(s p) f -> p s f", p=128)[:, t * NS:(t + 1) * NS, :], vs[i][:, :, :], reads=rvs[i])

            pg.enabled = True
            scr_done = [pg.dma_last[k] for k in list(pg.dma_last) if k[:3] in ('krs', 'szs', 'qns', 'qrs', 'kns') or k[:2] == 'vs']

        pg.barrier()
        rscr = Res("scr")
        w = []
        for ev in scr_done:
            pg._need('sp', ev, w)
        pg.ops['sp'].append(dict(waits=w, fn=None, ev=None, dma=None))

        if not do_a2:
            return pg.emit()
        with ExitStack() as st2:
            def sb2(name, shape, dt):
                return st2.enter_context(nc.sbuf_tensor(f"{tag}_{name}", shape, dt))
            KR = sb2("KR", [64, S], BF16)
            K = [sb2(f"K{i}", [128, S], BF16) for i in range(2)]
            V = [sb2(f"V{i}", [128, NB, 128], BF16) for i in range(2)]
            Qn = [sb2(f"Qn{i}", [128, TQ], BF16) for i in range(2)]
            Qr = [sb2(f"Qr{i}", [64, TQ], BF16) for i in range(2)]
            SZ = [sb2(f"SZ{i}", [128, TQ], BF16) for i in range(2)]
            PT = [sb2(f"PT{i}", [128, TQ], BF16) for i in range(3)]
            rec = sb2("rec", [128, TQ], F32)
            of = sb2("of", [128, TQ], F32)
            gs = [sb2(f"gs{i}", [128, TQ], BF16) for i in range(2)]
            rKR = Res("KR")
            rK = [Res("K0"), Res("K1")]
            rV = [Res("V0"), Res("V1")]
            rQ = [Res("Q0"), Res("Q1")]
            rPT = [Res(f"PT{i}") for i in range(3)]
            rrec, rof = Res("rec"), Res("of")
            rgs = [Res("gs0"), Res("gs1")]
            pg.dma('sp', 'KR', KR[:, :], scr['kr'][:, :], writes=[rKR])

            def loadhead(h):
                i = h % 2
                pg.dma('sp', f'K{i}', K[i][:, :], scr['kn'][h, :, :], writes=[rK[i]])
                pg.dma('sp', f'V{i}', V[i][:, :, :], scr['v'].rearrange("(b p) f -> p b f", p=128)[:, :, h * 128:(h + 1) * 128], writes=[rV[i]])

            qblocks = [(h, j) for h in range(NH) for j in range(nt)]

            def loadq(qi):
                h, j = qblocks[qi]
                i = qi % 2
                sl = slice(j * TQ, (j + 1) * TQ)
                pg.dma('sp', f'Qn{i}', Qn[i][:, :], scr['qn'][h, :, sl], writes=[rQ[i]])
                pg.dma('sp', f'Qr{i}', Qr[i][:, :], scr['qr'][h, :, sl], writes=[rQ[i]])
                pg.dma('sp', f'SZ{i}', SZ[i][:, :], scr['sz'][h * 128:(h + 1) * 128, sl], writes=[rQ[i]])

            items = []
            for qi, (h, j) in enumerate(qblocks):
                nkb = 4 * (j + 1)
                for kb in range(nkb):
                    items.append((qi, h, j, kb, nkb))

            loadhead(0)
            loadq(0)
            if NH > 1:
                loadhead(1)
            if len(qblocks) > 1:
                loadq(1)

            def emit_S(n):
                qi, h, j, kb, nkb = items[n]
                hi, qb = h % 2, qi % 2
                o = kb - 4 * j
                qoff = 0 if o <= 0 else o * 128
                b = n % 3
                pg.op('pe', lambda e: e.matmul(PS[b][:, qoff:TQ], K[hi][:, kb * 128:(kb + 1) * 128], Qn[qb][:, qoff:TQ], start=True, stop=False),
                      reads=[rK[hi], rQ[qb]], writes=[rPS[b]])
                pg.op('pe', lambda e: e.matmul(PS[b][:, qoff:TQ], KR[:, kb * 128:(kb + 1) * 128], Qr[qb][:, qoff:TQ], start=False, stop=True),
                      reads=[rKR, rQ[qb]], writes=[rPS[b]])
                pg.op('act', lambda e: e.activation(out=PT[b][:, qoff:TQ], in_=PS[b][:, qoff:TQ], func=AF.Exp, scale=float(scale)),
                      reads=[rPS[b]], writes=[rPT[b]])
                if o >= 0:
                    pg.op('pool', lambda e: e.memset(PT[b][64:128, qoff:qoff + 64], 0.0), reads=[], writes=[rPT[b]])

            def emit_PV(n):
                qi, h, j, kb, nkb = items[n]
                hi, qb = h % 2, qi % 2
                o = kb - 4 * j
                qoff = 0 if o <= 0 else o * 128
                b = n % 3
                bo = 3 + qi % 2
                bd = 5 + qi % 2
                pg.op('pe', lambda e: e.matmul(PS[bo][:, qoff:TQ], V[hi][:, kb, :], PT[b][:, qoff:TQ], start=(kb == 0), stop=(kb == nkb - 1)),
                      reads=[rV[hi], rPT[b]], writes=[rPS[bo]])
                pg.op('pe', lambda e: e.matmul(PS[bd][:, qoff:TQ], ones[:, :], PT[b][:, qoff:TQ], start=(kb == 0), stop=(kb == nkb - 1)),
                      reads=[rones, rPT[b]], writes=[rPS[bd]])
                if kb == nkb - 1:
                    pg.op('dve', lambda e: e.reciprocal(out=rec[:, :], in_=PS[bd][:, :]), reads=[rPS[bd]], writes=[rrec])
                    pg.op('dve', lambda e: e.tensor_tensor(out=of[:, :], in0=PS[bo][:, :], in1=rec[:, :], op=ALU.mult),
                          reads=[rPS[bo], rrec], writes=[rof])
                    pg.op('pool', lambda e: e.tensor_tensor(out=gs[qb][:, :], in0=of[:, :], in1=SZ[qb][:, :], op=ALU.mult),
                          reads=[rof, rQ[qb]], writes=[rgs[qb]])
                    pg.dma('sp', f'gs{qb}', gT_out[h * 128:(h + 1) * 128, j * TQ:(j + 1) * TQ], gs[qb][:, :], reads=[rgs[qb]])

            LOOK = 2
            for n in range(len(items) + LOOK):
                if n < len(items):
                    emit_S(n)
                if n >= LOOK:
                    emit_PV(n - LOOK)
                    qi_, h_, j_, kb_, nkb_ = items[n - LOOK]
                    if kb_ == nkb_ - 1:
                        if qi_ + 2 < len(qblocks):
                            loadq(qi_ + 2)
                        if j_ == nt - 1 and h_ + 2 < NH:
                            loadhead(h_ + 2)
        return pg.emit()

import math

EPS = 1e-6
NCH = 16
TN = 512
LNS = math.log(128.0 ** -0.5)


def phase_ML(nc, tag, S, NH, xT, Wqkd, Wvd, Wozd, convd, gbd, hnd, cst, ident_bf, mask_f, ones_f, scr, gT_out, dbg=None):
    pg = Prog(nc, tag)
    nt = S // TN
    NC = S // 128
    NQ = 2 * NH
    VW = NH * 256
    with ExitStack() as st:
        def sb(name, shape, dt):
            return st.enter_context(nc.sbuf_tensor(f"{tag}_{name}", shape, dt))

        PS = [st.enter_context(nc.psum_tensor(f"{tag}_ps{i}", [128, 512], F32)) for i in range(8)]
        rPS = [Res(f"ps{i}", excl=True) for i in range(8)]
        psc = [0]

        def nextps():
            b = psc[0] % 8
            psc[0] += 1
            return b

        C = sb("cst", [128, 4], F32)
        ident = sb("ident", [128, 128], BF16)
        mask = sb("mask", [128, 128], F32)
        onesf = sb("onesf", [128, 128], F32)
        gb = sb("gb", [128, 2 * NH], F32)
        hn = sb("hn", [128, VW], F32)
        cw = sb("cw", [128, NQ, 5], F32)
        GT = sb("GT", [128, NC, 2 * NH], F32)
        rC, rident, rmask, ronesf, rgb, rhn, rcw = (Res(n) for n in ("C", "ident", "mask", "onesf", "gb", "hn", "cw"))
        rGT = Res("GT")
        pg.dma('sp', 'cst', C[:, :], cst, writes=[rC])
        pg.dma('sp', 'ident', ident[:, :], ident_bf, writes=[rident])
        pg.dma('sp', 'mask', mask[:, :], mask_f, writes=[rmask])
        pg.dma('sp', 'onesf', onesf[:, :], ones_f, writes=[ronesf])
        pg.dma('sp', 'gb', gb[:, :], gbd, writes=[rgb])
        pg.dma('sp', 'hn', hn[:, :], hnd, writes=[rhn])
        pg.dma('sp', 'cw', cw[:, :, :], convd, writes=[rcw])
        xv = xT.rearrange("(c p) t -> p c t", p=128)

        with ExitStack() as st1:
            def sb1(name, shape, dt):
                return st1.enter_context(nc.sbuf_tensor(f"{tag}_{name}", shape, dt))
            Wqk = sb1("Wqk", [128, NCH, NQ * 128], BF16)
            Wv = sb1("Wv", [128, NCH, VW + 2 * NH], BF16)
            X = [sb1(f"X{i}", [128, NCH, TN], BF16) for i in range(2)]
            raw = [sb1(f"raw{i}", [128, NQ, TN + 3], F32) for i in range(2)]
            acc = [sb1(f"acc{i}", [128, TN], F32) for i in range(2)]
            qks = [sb1(f"qks{i}", [128, NQ, TN], BF16) for i in range(2)]
            vs = [sb1(f"vs{i}", [128, 4, VW], BF16) for i in range(2)]
            rWqk = [Res(f"Wqk{k}") for k in range(NCH)]
            rWv = [Res(f"Wv{k}") for k in range(NCH)]
            rX = [Res("X0"), Res("X1")]
            rraw = [[Res(f"raw{i}_{c}") for c in range(NQ)] for i in range(2)]
            rhalo = [[Res(f"halo{i}_{c}") for c in range(NQ)] for i in range(2)]
            racc = [Res("acc0"), Res("acc1")]
            rqks = [[Res(f"qks{i}_{c}") for c in range(NQ)] for i in range(2)]
            rvs = [[Res(f"vs{i}_{s}") for s in range(4)] for i in range(2)]

            wqkv = Wqkd.rearrange("(c p) n -> p c n", p=128)
            wvv = Wvd.rearrange("(c p) n -> p c n", p=128)
            for k in range(NCH):
                pg.dma('pool', f'Wa{k % 4}', Wqk[:, k, :], wqkv[:, k, :], writes=[rWqk[k]])
            for k in range(NCH):
                pg.dma('pool', f'Wb{k % 4}', Wv[:, k, :], wvv[:, k, :], writes=[rWv[k]])

            def loadx(t):
                i = t % 2
                sl = slice(t * TN, (t + 1) * TN)
                for q4 in range(4):
                    cs_ = slice(q4 * 4, (q4 + 1) * 4)
                    pg.dma('pool', f'X{i}', X[i][:, cs_, :], xv[:, cs_, sl], writes=[rX[i]])

            loadx(0)
            if dbg is not None:
                pg.dma('sp', 'dbgx', dbg['x'], X[0][:, :, :], reads=[rX[0]])
                pg.dma('sp', 'dbgw', dbg['wv'], Wv[:, :, :], reads=rWv)
                pg.dma('sp', 'dbgq', dbg['wqk'], Wqk[:, :, :], reads=rWqk)
            for c in range(NQ):
                pg.op('pool', lambda e, c=c: e.memset(raw[0][:, c, 0:3], 0.0), writes=[rhalo[0][c]])
            for t in range(nt):
                i = t % 2
                sl = slice(t * TN, (t + 1) * TN)
                if t + 1 < nt:
                    loadx(t + 1)
                for c in range(NQ):
                    b = nextps()
                    for k in range(NCH):
                        pg.op('pe', lambda e, b=b, k=k, c=c, i=i: e.matmul(
                            PS[b][:, :], Wqk[:, k, c * 128:(c + 1) * 128], X[i][:, k, :], start=(k == 0), stop=(k == NCH - 1)),
                            reads=[rWqk[k], rX[i]], writes=[rPS[b]])
                    pg.op('act', lambda e, b=b, c=c, i=i: e.activation(out=raw[i][:, c, 3:TN + 3], in_=PS[b][:, :], func=AF.Copy),
                          reads=[rPS[b]], writes=[rraw[i][c]])
                    pg.op('pool', lambda e, c=c, i=i: e.tensor_copy(out=raw[1 - i][:, c, 0:3], in_=raw[i][:, c, TN:TN + 3]),
                          reads=[rraw[i][c]], writes=[rhalo[1 - i][c]])
                    ab = c % 2
                    pg.op('dve', lambda e, c=c, i=i, ab=ab: e.tensor_scalar(out=acc[ab][:, :], in0=raw[i][:, c, 3:TN + 3], scalar1=cw[:, c, 3:4],
                                                                            scalar2=cw[:, c, 4:5], op0=ALU.mult, op1=ALU.add),
                          reads=[rraw[i][c], rcw], writes=[racc[ab]])
                    for j in (2, 1, 0):
                        pg.op('dve', lambda e, c=c, i=i, ab=ab, j=j: e.scalar_tensor_tensor(
                            out=acc[ab][:, :], in0=raw[i][:, c, j:TN + j], scalar=cw[:, c, j:j + 1], in1=acc[ab][:, :],
                            op0=ALU.mult, op1=ALU.add),
                            reads=[rraw[i][c], rhalo[i][c], rcw, racc[ab]], writes=[racc[ab]])
                    pg.op('act', lambda e, c=c, i=i, ab=ab: e.activation(out=qks[i][:, c, :], in_=acc[ab][:, :], func=AF.Silu),
                          reads=[racc[ab]], writes=[rqks[i][c]])
                pg.dma('sp', f'qks{i}', scr['qk'].rearrange("(c p) t -> p c t", p=128)[:, :, sl], qks[i][:, :, :], reads=rqks[i])
                for s in range(4):
                    for n in range(VW // 512):
                        b = nextps()
                        for k in range(NCH):
                            pg.op('pe', lambda e, b=b, k=k, s=s, n=n, i=i: e.matmul(
                                PS[b][:, :], X[i][:, k, s * 128:(s + 1) * 128], Wv[:, k, n * 512:(n + 1) * 512],
                                start=(k == 0), stop=(k == NCH - 1)),
                                reads=[rWv[k], rX[i]], writes=[rPS[b]])
                        pg.op('act', lambda e, b=b, s=s, n=n, i=i: e.activation(out=vs[i][:, s, n * 512:(n + 1) * 512], in_=PS[b][:, :], func=AF.Copy),
                              reads=[rPS[b]], writes=[rvs[i][s]])
                    b = nextps()
                    for k in range(NCH):
                        pg.op('pe', lambda e, b=b, k=k, s=s, i=i: e.matmul(
                            PS[b][:, 0:2 * NH], X[i][:, k, s * 128:(s + 1) * 128], Wv[:, k, VW:VW + 2 * NH],
                            start=(k == 0), stop=(k == NCH - 1)),
                            reads=[rWv[k], rX[i]], writes=[rPS[b]])
                    pg.op('dve', lambda e, b=b, s=s, t=t: e.tensor_tensor(out=GT[:, t * 4 + s, :], in0=PS[b][:, 0:2 * NH], in1=gb[:, :], op=ALU.add),
                          reads=[rPS[b], rgb], writes=[rGT])
                pg.dma('sp', f'vs{i}', scr['v'].rearrange("(s p) f -> p s f", p=128)[:, t * 4:(t + 1) * 4, :], vs[i][:, :, :], reads=rvs[i])

        pg.barrier()
        if dbg is not None and dbg.get('only1'):
            return pg.emit()
        with ExitStack() as st2:
            def sb2(name, shape, dt):
                return st2.enter_context(nc.sbuf_tensor(f"{tag}_{name}", shape, dt))
            Woz = sb2("Woz", [128, NCH, 2 * VW], BF16)
            X2 = [sb2(f"Xb{i}", [128, NCH, TN], BF16) for i in range(2)]
            so = sb2("so", [128, 4, VW], BF16)
            sz = sb2("sz", [128, 4, VW], BF16)
            ogs = [sb2(f"ogs{i}", [128, 4, VW], BF16) for i in range(2)]
            rWoz = [Res(f"Woz{k}") for k in range(NCH)]
            rX = [Res("Xb0"), Res("Xb1")]
            rso = [Res(f"so{s}") for s in range(4)]
            rsz = [Res(f"sz{s}") for s in range(4)]
            rogs = [[Res(f"ogs{i}_{s}") for s in range(4)] for i in range(2)]
            wozv = Wozd.rearrange("(c p) n -> p c n", p=128)
            for k in range(NCH):
                pg.dma('pool', f'Wa{k % 4}', Woz[:, k, :], wozv[:, k, :], writes=[rWoz[k]])

            def loadx2(t):
                i = t % 2
                sl = slice(t * TN, (t + 1) * TN)
                for q4 in range(4):
                    cs_ = slice(q4 * 4, (q4 + 1) * 4)
                    pg.dma('pool', f'X{i}', X2[i][:, cs_, :], xv[:, cs_, sl], writes=[rX[i]])

            loadx2(0)
            nv = VW // 512
            for t in range(nt):
                i = t % 2
                if t + 1 < nt:
                    loadx2(t + 1)
                for half in range(2):
                    for s in range(4):
                        for n in range(nv):
                            b = nextps()
                            col = half * VW + n * 512
                            for k in range(NCH):
                                pg.op('pe', lambda e, b=b, k=k, s=s, col=col, i=i: e.matmul(
                                    PS[b][:, :], X2[i][:, k, s * 128:(s + 1) * 128], Woz[:, k, col:col + 512],
                                    start=(k == 0), stop=(k == NCH - 1)),
                                    reads=[rWoz[k], rX[i]], writes=[rPS[b]])
                            if half == 0:
                                pg.op('act', lambda e, b=b, s=s, n=n: e.activation(out=so[:, s, n * 512:(n + 1) * 512], in_=PS[b][:, :], func=AF.Sigmoid),
                                      reads=[rPS[b]], writes=[rso[s]])
                            else:
                                pg.op('act', lambda e, b=b, s=s, n=n: e.activation(out=sz[:, s, n * 512:(n + 1) * 512], in_=PS[b][:, :], func=AF.Silu),
                                      reads=[rPS[b]], writes=[rsz[s]])
                for s in range(4):
                    pg.op('pool', lambda e, s=s, i=i: e.tensor_tensor(out=ogs[i][:, s, :], in0=so[:, s, :], in1=sz[:, s, :], op=ALU.mult),
                          reads=[rso[s], rsz[s]], writes=[rogs[i][s]])
                pg.dma('sp', f'ogs{i}', scr['og'].rearrange("(s p) f -> p s f", p=128)[:, t * 4:(t + 1) * 4, :], ogs[i][:, :, :], reads=rogs[i])

        w = []
        for k_, ev in list(pg.dma_last.items()):
            if k_[:3] in ('qks', 'ogs') or k_[:2] == 'vs':
                pg._need('sp', ev, w)
        pg.ops['sp'].append(dict(waits=w, fn=None, ev=None, dma=None))

        pg.barrier()
        with ExitStack() as st3:
            def sb3(name, shape, dt):
                return st3.enter_context(nc.sbuf_tensor(f"{tag}_{name}", shape, dt))
            NG = NC * NH
            Lg = sb3("Lg", [128, NC * NH], F32)
            tmpg = sb3("tmpg", [128, NC * NH], F32)
            U = sb3("U", [128, NC * NH], F32)
            WK = sb3("WK", [128, NC * NH], F32)
            ET = sb3("ET", [128, NC * NH], F32)
            EB = sb3("EB", [128, NC * NH], F32)
            rLg, rtmpg, rU, rWK, rET, rEB = (Res(n) for n in ("Lg", "tmpg", "U", "WK", "ET", "EB"))
            v3 = lambda ap: ap.rearrange("p (c h) -> p c h", h=NH)
            pg.op('act', lambda e: e.activation(out=v3(Lg[:, :]), in_=GT[:, :, NH:2 * NH], func=AF.Exp, scale=-1.0), reads=[rGT], writes=[rLg])
            pg.op('act', lambda e: e.activation(out=Lg[:, :], in_=Lg[:, :], func=AF.Ln, bias=C[:, 0:1]), reads=[rLg, rC], writes=[rLg])
            bc, bl = nextps(), nextps()
            pg.op('pe', lambda e: e.matmul(PS[bc][:, 0:NG], mask[:, :], Lg[:, :], start=True, stop=True), reads=[rmask, rLg], writes=[rPS[bc]])
            pg.op('pe', lambda e: e.matmul(PS[bl][:, 0:NG], onesf[:, :], Lg[:, :], start=True, stop=True), reads=[ronesf, rLg], writes=[rPS[bl]])
            pg.op('dve', lambda e: e.tensor_tensor(out=v3(tmpg[:, :]), in0=GT[:, :, 0:NH], in1=v3(PS[bc][:, 0:NG]), op=ALU.add), reads=[rGT, rPS[bc]], writes=[rtmpg])
            pg.op('act', lambda e: e.activation(out=U[:, :], in_=tmpg[:, :], func=AF.Exp, bias=C[:, 1:2]), reads=[rtmpg, rC], writes=[rU])
            pg.op('act', lambda e: e.activation(out=ET[:, :], in_=PS[bc][:, 0:NG], func=AF.Exp), reads=[rPS[bc]], writes=[rET])
            pg.op('act', lambda e: e.activation(out=EB[:, :], in_=PS[bl][:, 0:NG], func=AF.Exp, scale=-1.0), reads=[rPS[bl]], writes=[rEB])
            pg.op('dve', lambda e: e.tensor_tensor(out=WK[:, :], in0=EB[:, :], in1=U[:, :], op=ALU.mult), reads=[rEB, rU], writes=[rWK])

            S32 = [sb3(f"S32_{h}", [128, 257], F32) for h in range(NH)]
            Sbf = [sb3(f"Sbf_{h}", [128, 257], BF16) for h in range(NH)]
            QK = [sb3(f"QK{i}", [128, NQ, TN], BF16) for i in range(2)]
            Vp = [sb3(f"Vp{i}", [128, 4, NH, 257], BF16) for i in range(2)]
            OG = [sb3(f"OG{i}", [128, 4, VW], BF16) for i in range(2)]
            Asb = [sb3(f"Asb{i}", [128, 128], BF16) for i in range(2)]
            Kw = [sb3(f"Kw{i}", [128, 128], BF16) for i in range(2)]
            sm = [sb3(f"sm{i}", [128, 8], F32) for i in range(2)]
            junk = sb3("junk", [128, 256], F32)
            hv = [sb3(f"hv{i}", [128, 256], F32) for i in range(2)]
            gtm = [sb3(f"gtm{i}", [128, 256], BF16) for i in range(2)]
            gst = [sb3(f"gst{i}", [128, 2 * NH, TN], BF16) for i in range(2)]
            rS32 = [Res(f"S32_{h}") for h in range(NH)]
            rSbf = [Res(f"Sbf_{h}") for h in range(NH)]
            rQK = [Res("QK0"), Res("QK1")]
            rVp = [Res("Vp0"), Res("Vp1")]
            rOG = [Res("OG0"), Res("OG1")]
            rAsb = [Res("A0"), Res("A1")]
            rKw = [Res("Kw0"), Res("Kw1")]
            rsm = [Res("sm0"), Res("sm1")]
            rjunk = Res("junk")
            rhv = [Res("hv0"), Res("hv1")]
            rgtm = [Res("gtm0"), Res("gtm1")]
            rgst = [[Res(f"gst{i}_{c}") for c in range(4)] for i in range(2)]
            for h in range(NH):
                pg.op('pool', lambda e, h=h: e.memset(S32[h][:, :], 0.0), writes=[rS32[h]])
                pg.op('pool', lambda e, h=h: e.memset(Sbf[h][:, :], 0.0), writes=[rSbf[h]])
            for i in range(2):
                pg.op('pool', lambda e, i=i: e.memset(Vp[i][:, :, :, 256:257], 1.0), writes=[rVp[i]])

            qkv = scr['qk'].rearrange("(c p) t -> p c t", p=128)
            vv = scr['v'].rearrange("(s p) (h f) -> p s h f", p=128, h=NH)
            ogv = scr['og'].rearrange("(s p) f -> p s f", p=128)

            def loadst(t):
                i = t % 2
                sl = slice(t * TN, (t + 1) * TN)
                pg.dma('sp', f'QK{i}', QK[i][:, :, :], qkv[:, :, sl], writes=[rQK[i]])
                for s in range(4):
                    pg.dma('sp', f'Vp{i}_{s}', Vp[i][:, s, :, 0:256], vv[:, t * 4 + s, :, :], writes=[rVp[i]])
                pg.dma('sp', f'OG{i}', OG[i][:, :, :], ogv[:, t * 4:(t + 1) * 4, :], writes=[rOG[i]])

            PK = PS[2][:, 0:64].bitcast(BF16)
            PT = PS[7][:, 0:128].bitcast(BF16)
            items = [(c, h) for c in range(NC) for h in range(NH)]

            def stage1(n):
                c, h = items[n]
                t, cc = c // 4, c % 4
                i = t % 2
                p = n % 2
                csl = slice(cc * 128, (cc + 1) * 128)
                pg.op('pe', lambda e: e.matmul(PS[p][:, 0:128], QK[i][:, NH + h, csl], QK[i][:, h, csl], start=True, stop=True),
                      reads=[rQK[i]], writes=[rPS[p]])
                pg.op('dve', lambda e: e.scalar_tensor_tensor(out=Asb[p][:, :], in0=PS[p][:, 0:128], scalar=U[:, c * NH + h:c * NH + h + 1], in1=mask[:, :],
                                                              op0=ALU.mult, op1=ALU.mult),
                      reads=[rPS[p], rU, rmask], writes=[rAsb[p]])
                pg.op('pe', lambda e: e.transpose(PK, QK[i][:, NH + h, csl], ident[:, :]), reads=[rQK[i], rident], writes=[rPS[2]])
                pg.op('act', lambda e: e.activation(out=Kw[p][:, :], in_=PK, func=AF.Copy, scale=WK[:, c * NH + h:c * NH + h + 1]),
                      reads=[rPS[2], rWK], writes=[rKw[p]])

            def stage2(n):
                c, h = items[n]
                t, cc = c // 4, c % 4
                i = t % 2
                p = n % 2
                csl = slice(cc * 128, (cc + 1) * 128)
                bp, bcn = 3 + p, 5 + p
                pg.op('pe', lambda e: e.matmul(PS[bp][:, 0:257], QK[i][:, h, csl], Sbf[h][:, :], start=True, stop=False),
                      reads=[rQK[i], rSbf[h]], writes=[rPS[bp]])
                pg.op('pe', lambda e: e.matmul(PS[bp][:, 0:257], Asb[p][:, :], Vp[i][:, cc, h, :], start=False, stop=True),
                      reads=[rAsb[p], rVp[i]], writes=[rPS[bp]])
                pg.op('pe', lambda e: e.matmul(PS[bcn][:, 0:257], Kw[p][:, :], Vp[i][:, cc, h, :], start=True, stop=True),
                      reads=[rKw[p], rVp[i]], writes=[rPS[bcn]])
                pg.op('dve', lambda e: e.scalar_tensor_tensor(out=S32[h][:, :], in0=S32[h][:, :], scalar=EB[:, c * NH + h:c * NH + h + 1], in1=PS[bcn][:, 0:257],
                                                              op0=ALU.mult, op1=ALU.add),
                      reads=[rS32[h], rEB, rPS[bcn]], writes=[rS32[h]])
                pg.op('pool', lambda e: e.tensor_copy(out=Sbf[h][:, :], in_=S32[h][:, :]), reads=[rS32[h]], writes=[rSbf[h]])
                pg.op('dve', lambda e: e.tensor_copy(out=sm[p][:, 6:7], in_=PS[bp][:, 256:257]), reads=[rPS[bp]], writes=[rsm[p]])
                pg.op('dve', lambda e: e.scalar_tensor_tensor(out=sm[p][:, 0:1], in0=sm[p][:, 6:7], scalar=-1.0, in1=sm[p][:, 6:7],
                                                              op0=ALU.mult, op1=ALU.max),
                      reads=[rsm[p]], writes=[rsm[p]])
                pg.op('dve', lambda e: e.tensor_tensor(out=sm[p][:, 1:2], in0=sm[p][:, 0:1], in1=ET[:, c * NH + h:c * NH + h + 1], op=ALU.max),
                      reads=[rsm[p], rET], writes=[rsm[p]])
                pg.op('dve', lambda e: e.reciprocal(out=sm[p][:, 2:3], in_=sm[p][:, 1:2]), reads=[rsm[p]], writes=[rsm[p]])
                pg.op('act', lambda e: e.activation(out=hv[p][:, :], in_=PS[bp][:, 0:256], func=AF.Copy, scale=sm[p][:, 2:3]),
                      reads=[rPS[bp], rsm[p]], writes=[rhv[p]])
                pg.op('act', lambda e: e.activation(out=junk[:, :], in_=hv[p][:, :], func=AF.Square),
                      reads=[rhv[p]], writes=[rjunk])
                pg.op('dve', lambda e: e.tensor_reduce(out=sm[p][:, 3:4], in_=junk[:, :], axis=AX.X, op=ALU.add),
                      reads=[rjunk], writes=[rsm[p]])
                pg.op('act', lambda e: e.activation(out=sm[p][:, 4:5], in_=sm[p][:, 3:4], func=AF.Ln, scale=1.0 / 256, bias=C[:, 2:3]),
                      reads=[rsm[p], rC], writes=[rsm[p]])
                pg.op('act', lambda e: e.activation(out=sm[p][:, 5:6], in_=sm[p][:, 4:5], func=AF.Exp, scale=-0.5),
                      reads=[rsm[p]], writes=[rsm[p]])
                pg.op('dve', lambda e: e.scalar_tensor_tensor(out=hv[p][:, :], in0=hv[p][:, :], scalar=sm[p][:, 5:6], in1=hn[:, h * 256:(h + 1) * 256],
                                                              op0=ALU.mult, op1=ALU.mult),
                      reads=[rhv[p], rsm[p], rhn], writes=[rhv[p]])
                pg.op('pool', lambda e: e.tensor_tensor(out=gtm[p][:, :], in0=hv[p][:, :], in1=OG[i][:, cc, h * 256:(h + 1) * 256], op=ALU.mult),
                      reads=[rhv[p], rOG[i]], writes=[rgtm[p]])

            def stage3(n):
                c, h = items[n]
                t, cc = c // 4, c % 4
                i = t % 2
                p = n % 2
                csl = slice(cc * 128, (cc + 1) * 128)
                for e2 in range(2):
                    pg.op('pe', lambda e, e2=e2: e.transpose(PT[:, e2 * 128:(e2 + 1) * 128], gtm[p][:, e2 * 128:(e2 + 1) * 128], ident[:, :]),
                          reads=[rgtm[p], rident], writes=[rPS[7]])
                pg.op('act', lambda e: e.activation(out=gst[i][:, 2 * h:2 * h + 2, csl], in_=PT.rearrange("p (a b) -> p a b", a=2), func=AF.Copy),
                      reads=[rPS[7]], writes=[rgst[i][cc]])
                if cc == 3 and h == NH - 1:
                    sl = slice(t * TN, (t + 1) * TN)
                    pg.dma('sp', f'gst{i}', gT_out.rearrange("(c p) t -> p c t", p=128)[:, :, sl], gst[i][:, :, :], reads=rgst[i])
                    if t + 2 < nt:
                        loadst(t + 2)

            loadst(0)
            if nt > 1:
                loadst(1)
            N = len(items)
            for n in range(N + 2):
                if n < N:
                    stage1(n)
                if 1 <= n <= N:
                    stage2(n - 1)
                if n >= 2:
                    stage3(n - 2)
        return pg.emit()

import numpy as np, math
PERM = np.concatenate([np.arange(32, 64), np.arange(0, 32)])

def mla_consts():
    c = np.zeros((128, 4), np.float32)
    invf = (10000.0 ** (-np.arange(0, 64, 2, dtype=np.float32) / 64)).astype(np.float32)
    c[0:32, 0] = invf; c[32:64, 0] = invf
    c[0:32, 1] = -1.0; c[32:64, 1] = 1.0
    c[:, 2] = math.pi / 2
    c[:, 3] = 1e-6
    return c

def prep_mla(w_in, w_qb, w_kvb, qnorm, kvnorm, heads):
    kr = w_in[:, 1024:1088]
    W1 = np.concatenate([w_in[:, 0:1024], kr, kr[:, PERM]] + [w_in[:, 1088 + h * 128:1088 + (h + 1) * 128] for h in heads], axis=1)
    qs = []
    for h in heads:
        nope = w_qb[:, h * 192:h * 192 + 128]; rope = w_qb[:, h * 192 + 128:h * 192 + 192]
        qs += [nope, rope, rope[:, PERM]]
    Wq = np.concatenate(qs, axis=1)
    Wkv = np.concatenate([w_kvb[:, h * 256:h * 256 + 128] for h in heads] + [w_kvb[:, h * 256 + 128:h * 256 + 256] for h in heads], axis=1)
    qn = np.ascontiguousarray(qnorm.reshape(4, 128).T)
    kvn = np.ascontiguousarray(kvnorm.reshape(4, 128).T)
    return dict(W1=np.ascontiguousarray(W1), Wq=np.ascontiguousarray(Wq), Wkv=np.ascontiguousarray(Wkv), qn=qn, kvn=kvn)


def ml_consts():
    c = np.zeros((128, 4), np.float32)
    c[:, 0] = 1.0; c[:, 1] = math.log(128.0 ** -0.5); c[:, 2] = 1e-6
    return c

def prep_ml(w_in, conv_w, conv_b, gate_b, head_norm, heads):
    nh = len(heads)
    q = [w_in[:, h * 128:(h + 1) * 128] for h in heads]
    k = [w_in[:, 1024 + h * 128:1024 + (h + 1) * 128] for h in heads]
    Wqk = np.concatenate(q + k, axis=1)
    v = [w_in[:, 2048 + h * 256:2048 + (h + 1) * 256] for h in heads]
    gi = [w_in[:, 6144 + h:6144 + h + 1] for h in heads]
    gf = [w_in[:, 6144 + 8 + h:6144 + 8 + h + 1] for h in heads]
    Wv = np.concatenate(v + gi + gf, axis=1)
    o = [w_in[:, 4096 + h * 256:4096 + (h + 1) * 256] for h in heads]
    z = [w_in[:, 6160 + h * 256:6160 + (h + 1) * 256] for h in heads]
    Woz = np.concatenate(o + z, axis=1)
    cols = [np.arange(h * 128, (h + 1) * 128) for h in heads] + [1024 + np.arange(h * 128, (h + 1) * 128) for h in heads]
    conv = np.zeros((128, 2 * nh, 5), np.float32)
    for c, cc in enumerate(cols):
        conv[:, c, 0:4] = conv_w[:, cc].T
        conv[:, c, 4] = conv_b[cc]
    gb = np.concatenate([gate_b[list(heads)], gate_b[[8 + h for h in heads]]])
    gbb = np.ascontiguousarray(np.broadcast_to(gb[None, :], (128, 2 * nh))).astype(np.float32)
    hn = np.concatenate([head_norm[h * 256:(h + 1) * 256] for h in heads])
    hnb = np.ascontiguousarray(np.broadcast_to(hn[None, :], (128, nh * 256))).astype(np.float32)
    return dict(Wqk=np.ascontiguousarray(Wqk), Wv=np.ascontiguousarray(Wv), Woz=np.ascontiguousarray(Woz), conv=conv, gb=gbb, hn=hnb)


import ml_dtypes
from concourse.bass_utils import run_bass_kernel_spmd

S_FULL = 8192
NB = 4
_PROGS = {}


def _din(nc, name, shape, dt):
    return nc.dram_tensor(name, list(shape), dt, kind="ExternalInput").ap()


def _build_mla():
    nc = bass.Bass("TRN2", target_bir_lowering=False)
    S, NH = S_FULL, 8
    xT = _din(nc, "xT", [2048, S], F32)
    pos = _din(nc, "pos", [S], I32)
    W1 = _din(nc, "W1", [2048, 1152 + NH * 128], F32)
    Wq = _din(nc, "Wq", [512, NH * 256], F32)
    Wkv = _din(nc, "Wkv", [512, NH * 256], F32)
    qn = _din(nc, "qn", [128, 4], F32)
    kvn = _din(nc, "kvn", [128, 4], F32)
    cst = _din(nc, "cst", [128, 4], F32)
    ones = _din(nc, "ones", [128, 128], BF16)
    scr = dict(qn=nc.dram_tensor("s_qn", [NH, 128, S], BF16).ap(), qr=nc.dram_tensor("s_qr", [NH, 64, S], BF16).ap(),
               kn=nc.dram_tensor("s_kn", [NH, 128, S], BF16).ap(), kr=nc.dram_tensor("s_kr", [64, S], BF16).ap(),
               v=nc.dram_tensor("s_v", [S, NH * 128], BF16).ap(), sz=nc.dram_tensor("s_sz", [NH * 128, S], BF16).ap())
    gT = nc.dram_tensor("gT", [NH * 128, S], BF16, kind="ExternalOutput").ap()
    phase_MLA(nc, "A", S, NH, xT, pos, W1, Wq, Wkv, qn, kvn, cst, ones, scr, gT)
    return nc


def _build_ml():
    nc = bass.Bass("TRN2", target_bir_lowering=False)
    S, NH = S_FULL, 4
    xT = _din(nc, "xT", [2048, S], F32)
    Wqk = _din(nc, "Wqk", [2048, 2 * NH * 128], F32)
    Wv = _din(nc, "Wv", [2048, NH * 256 + 2 * NH], F32)
    Woz = _din(nc, "Woz", [2048, 2 * NH * 256], F32)
    conv = _din(nc, "conv", [128, 2 * NH, 5], F32)
    gb = _din(nc, "gb", [128, 2 * NH], F32)
    hn = _din(nc, "hn", [128, NH * 256], F32)
    cst = _din(nc, "cst", [128, 4], F32)
    ident = _din(nc, "ident", [128, 128], BF16)
    mask = _din(nc, "mask", [128, 128], F32)
    onesf = _din(nc, "onesf", [128, 128], F32)
    scr = dict(qk=nc.dram_tensor("s_qk", [2 * NH * 128, S], BF16).ap(), v=nc.dram_tensor("s_v", [S, NH * 256], BF16).ap(),
               og=nc.dram_tensor("s_og", [S, NH * 256], BF16).ap())
    gT = nc.dram_tensor("gT", [NH * 256, S], BF16, kind="ExternalOutput").ap()
    phase_ML(nc, "M", S, NH, xT, Wqk, Wv, Woz, conv, gb, hn, cst, ident, mask, onesf, scr, gT)
    return nc


def _build_B():
    nc = bass.Bass("TRN2", target_bir_lowering=False)
    S = S_FULL // 2
    gT = _din(nc, "gT", [2048, S], BF16)
    xT = _din(nc, "xT", [2048, S], F32)
    wo = _din(nc, "wo", [2048, 2048], F32)
    lnw = _din(nc, "lnw", [128, 16], F32)
    lnb = _din(nc, "lnb", [128, 16], F32)
    ones = _din(nc, "ones", [128, 128], BF16)
    oT = nc.dram_tensor("oT", [2048, S], F32, kind="ExternalOutput").ap()
    phase_B(nc, "B", S, gT, xT, wo, lnw, lnb, ones, oT)
    return nc


def _prog(kind):
    if kind not in _PROGS:
        _PROGS[kind] = {"mla": _build_mla, "ml": _build_ml, "B": _build_B}[kind]()
    return _PROGS[kind]


def kernel(x, positions, mla_w_in, mla_q_norm, mla_w_qb, mla_kv_norm, mla_w_kvb, mla_w_out,
           ml_w_in, ml_conv_w, ml_conv_b, ml_gate_b, ml_head_norm, ml_w_out, ln_w, ln_b):
    x = np.asarray(x)
    positions = np.asarray(positions).astype(np.int32)
    S = S_FULL
    H = S // 2
    cores = list(range(8))
    xT = [np.ascontiguousarray(x[b].T) for b in range(NB)]
    ones_bf = np.ones((128, 128), ml_dtypes.bfloat16)
    ident_bf = np.eye(128, dtype=np.float32).astype(ml_dtypes.bfloat16)
    mask_f = np.triu(np.ones((128, 128), np.float32))
    ones_f = np.ones((128, 128), np.float32)
    for layer in range(4):
        j = layer // 2
        if layer % 2 == 0:
            preps = [prep_mla(np.asarray(mla_w_in[j]), np.asarray(mla_w_qb[j]), np.asarray(mla_w_kvb[j]), np.asarray(mla_q_norm[j]),
                              np.asarray(mla_kv_norm[j]), list(range(8 * g, 8 * g + 8))) for g in range(2)]
            cst = mla_consts()
            in_maps = []
            for c in cores:
                b, g = c // 2, c % 2
                p = preps[g]
                in_maps.append({"xT": xT[b], "pos": np.ascontiguousarray(positions[b]), "W1": p['W1'], "Wq": p['Wq'], "Wkv": p['Wkv'],
                                "qn": p['qn'], "kvn": p['kvn'], "cst": cst, "ones": ones_bf})
            res = run_bass_kernel_spmd(_prog("mla"), in_maps, core_ids=cores)
            w_out = np.asarray(mla_w_out[j])
        else:
            preps = [prep_ml(np.asarray(ml_w_in[j]), np.asarray(ml_conv_w[j]), np.asarray(ml_conv_b[j]), np.asarray(ml_gate_b[j]),
                             np.asarray(ml_head_norm[j]), list(range(4 * g, 4 * g + 4))) for g in range(2)]
            cst = ml_consts()
            in_maps = []
            for c in cores:
                b, g = c // 2, c % 2
                p = preps[g]
                in_maps.append({"xT": xT[b], "Wqk": p['Wqk'], "Wv": p['Wv'], "Woz": p['Woz'], "conv": p['conv'], "gb": p['gb'], "hn": p['hn'],
                                "cst": cst, "ident": ident_bf, "mask": mask_f, "onesf": ones_f})
            res = run_bass_kernel_spmd(_prog("ml"), in_maps, core_ids=cores)
            w_out = np.asarray(ml_w_out[j])
        g_all = [np.asarray(res.results[c]["gT"]) for c in cores]
        lnw = np.ascontiguousarray(np.asarray(ln_w[layer]).reshape(16, 128).T)
        lnb = np.ascontiguousarray(np.asarray(ln_b[layer]).reshape(16, 128).T)
        in_maps = []
        for c in cores:
            b, g = c // 2, c % 2
            sl = slice(g * H, (g + 1) * H)
            gfull = np.concatenate([g_all[2 * b][:, sl], g_all[2 * b + 1][:, sl]], axis=0)
            in_maps.append({"gT": np.ascontiguousarray(gfull), "xT": np.ascontiguousarray(xT[b][:, sl]), "wo": w_out, "lnw": lnw, "lnb": lnb,
                            "ones": ones_bf})
        res = run_bass_kernel_spmd(_prog("B"), in_maps, core_ids=cores)
        xT = [np.concatenate([np.asarray(res.results[2 * b]["oT"]), np.asarray(res.results[2 * b + 1]["oT"])], axis=1) for b in range(NB)]
    out = np.stack([np.ascontiguousarray(xT[b].T) for b in range(NB)], axis=0).astype(np.float32)
    return out
```
